# Optimizing a Trainium2 kernel written in Bass

```python
import jax, jax.numpy as jnp
from jax import lax
import numpy as np

D_MODEL = 2048
BATCH = 2
SEQ = 8192
DEPTH = 2

CHUNK = 64
N_MEM = 256
EXPAND = 2
MIX_WIDTH = EXPAND * D_MODEL
W_A = MIX_WIDTH // 2
HEAD_DIM_A = 128
N_HEADS_A = W_A // HEAD_DIM_A
N_PAST_CHUNKS = 8
MAX_REL = 128
W_B = MIX_WIDTH - W_A
CONV_WIDTH = 31
GMLP_CHUNK = 128
N_GROUPS_C = 8
N_HEADS_X = 4
HEAD_DIM_X = D_MODEL // N_HEADS_X
EPS = 1e-6
N_EVEN = (DEPTH + 1) // 2
N_ODD = DEPTH // 2
AB_IN_COLS = 3 * W_A + 2 * W_B + MIX_WIDTH
C_IN_COLS = 3 * MIX_WIDTH

kernel_name = "hybrid_streaming_band_conv_sgu_encoder"


def rmsnorm(x, g):
    xf = x.astype(jnp.float32)
    y = xf * lax.rsqrt(jnp.mean(xf * xf, axis=-1, keepdims=True) + EPS)
    return (y * g.astype(jnp.float32)).astype(x.dtype)


def layernorm(x, g, b):
    xf = x.astype(jnp.float32)
    mu = jnp.mean(xf, axis=-1, keepdims=True)
    var = jnp.mean(jnp.square(xf - mu), axis=-1, keepdims=True)
    y = (xf - mu) * lax.rsqrt(var + EPS)
    return (y * g.astype(jnp.float32) + b.astype(jnp.float32)).astype(x.dtype)


def chunk_band_attention(q, k, v, rel_bias):
    b, s, h, dh = q.shape
    n_chunks = s // CHUNK
    pad = N_PAST_CHUNKS * CHUNK
    band = (N_PAST_CHUNKS + 1) * CHUNK
    k_pad = jnp.pad(k, ((0, 0), (pad, 0), (0, 0), (0, 0)))
    v_pad = jnp.pad(v, ((0, 0), (pad, 0), (0, 0), (0, 0)))
    q_off = np.arange(CHUNK)
    k_off = np.arange(band) - pad
    rel_idx = np.clip(q_off[:, None] - k_off[None, :], -MAX_REL, MAX_REL) + MAX_REL
    bias = jnp.take(rel_bias.astype(jnp.float32), jnp.asarray(rel_idx), axis=1)
    scale = dh ** -0.5
    k_off_j = jnp.asarray(k_off)

    def one_chunk(c):
        start = c * CHUNK
        qc = lax.dynamic_slice_in_dim(q, start, CHUNK, axis=1)
        kb = lax.dynamic_slice_in_dim(k_pad, start, band, axis=1)
        vb = lax.dynamic_slice_in_dim(v_pad, start, band, axis=1)
        sc = jnp.einsum('bqhd,bkhd->bhqk', qc, kb).astype(jnp.float32) * scale + bias[None]
        valid = (start + k_off_j) >= 0
        sc = jnp.where(valid[None, None, None, :], sc, jnp.float32(-1e30))
        p = jax.nn.softmax(sc, axis=-1).astype(vb.dtype)
        return jnp.einsum('bhqk,bkhd->bqhd', p, vb)

    out = lax.map(one_chunk, jnp.arange(n_chunks))
    return jnp.transpose(out, (1, 0, 2, 3, 4)).reshape(b, s, h * dh)


def causal_depthwise_conv(x, w, bias):
    c = x.shape[-1]
    xp = jnp.pad(x, ((0, 0), (w.shape[0] - 1, 0), (0, 0)))
    y = lax.conv_general_dilated(xp, w[:, None, :], window_strides=(1,), padding='VALID',
                                 dimension_numbers=('NWC', 'WIO', 'NWC'),
                                 feature_group_count=c)
    return y + bias


def ab_mixer(hn, w_in, rel_bias, conv_w, conv_b, ln_g, ln_b, w_out):
    b, s, _ = hn.shape
    proj = hn @ w_in
    splits = [W_A, 2 * W_A, 3 * W_A, 3 * W_A + W_B, 3 * W_A + 2 * W_B]
    q, k, v, glu_a, glu_b, gate = jnp.split(proj, splits, axis=-1)
    shp = (b, s, N_HEADS_A, HEAD_DIM_A)
    ya = chunk_band_attention(q.reshape(shp), k.reshape(shp), v.reshape(shp), rel_bias)
    yb = glu_a * jax.nn.sigmoid(glu_b)
    yb = jax.nn.silu(layernorm(causal_depthwise_conv(yb, conv_w, conv_b), ln_g, ln_b))
    y = jnp.concatenate([ya, yb], axis=-1) * jax.nn.silu(gate)
    return y @ w_out


def c_mixer(hn, w_in, ln_g, ln_b, w_s, b_s, w_out):
    b, s, _ = hn.shape
    u, v, gate = jnp.split(hn @ w_in, [MIX_WIDTH, 2 * MIX_WIDTH], axis=-1)
    v = layernorm(v, ln_g, ln_b)
    n_blk = s // GMLP_CHUNK
    vr = v.reshape(b, n_blk, GMLP_CHUNK, N_GROUPS_C, MIX_WIDTH // N_GROUPS_C)
    pos_chunk = np.arange(GMLP_CHUNK) // CHUNK
    mask = jnp.asarray(pos_chunk[:, None] >= pos_chunk[None, :], dtype=w_s.dtype)
    ws = w_s * mask[None]
    sg = jnp.einsum('gij,bnjgc->bnigc', ws, vr) + jnp.transpose(b_s)[None, None, :, :, None]
    y = u * sg.reshape(b, s, MIX_WIDTH) * jax.nn.silu(gate)
    return y @ w_out


def memory_cross_attention(hn, mem_n, wq, wk, wv, wo):
    b, s, _ = hn.shape
    q = (hn @ wq).reshape(b, s, N_HEADS_X, HEAD_DIM_X)
    k = (mem_n @ wk).reshape(b, N_MEM, N_HEADS_X, HEAD_DIM_X)
    v = (mem_n @ wv).reshape(b, N_MEM, N_HEADS_X, HEAD_DIM_X)
    sc = jnp.einsum('bqhd,bkhd->bhqk', q, k).astype(jnp.float32) * (HEAD_DIM_X ** -0.5)
    p = jax.nn.softmax(sc, axis=-1).astype(v.dtype)
    o = jnp.einsum('bhqk,bkhd->bqhd', p, v).reshape(b, s, D_MODEL)
    return o @ wo


def setup_inputs(seed: int = 0) -> dict:
    key = jax.random.key(seed)
    ks = iter(jax.random.split(key, 32))
    nrm = lambda shape, scale: jax.random.normal(next(ks), shape, jnp.float32) * scale
    gain = lambda shape: 1.0 + nrm(shape, 0.01)
    d = D_MODEL
    return {
        "x": nrm((BATCH, SEQ, d), 1.0),
        "mem": nrm((BATCH, N_MEM, d), 1.0),
        "norm_mix_g": gain((DEPTH, d)),
        "norm_x_g": gain((DEPTH, d)),
        "norm_mem_g": gain((DEPTH, d)),
        "final_norm_g": gain((d,)),
        "w_in_ab": nrm((N_EVEN, d, AB_IN_COLS), d ** -0.5),
        "rel_bias": nrm((N_EVEN, N_HEADS_A, 2 * MAX_REL + 1), 0.2),
        "conv_w": nrm((N_EVEN, CONV_WIDTH, W_B), CONV_WIDTH ** -0.5),
        "conv_b": nrm((N_EVEN, W_B), 0.01),
        "conv_ln_g": gain((N_EVEN, W_B)),
        "conv_ln_b": nrm((N_EVEN, W_B), 0.01),
        "w_out_ab": nrm((N_EVEN, MIX_WIDTH, d), MIX_WIDTH ** -0.5),
        "w_in_c": nrm((N_ODD, d, C_IN_COLS), d ** -0.5),
        "sgu_ln_g": gain((N_ODD, MIX_WIDTH)),
        "sgu_ln_b": nrm((N_ODD, MIX_WIDTH), 0.01),
        "w_s": nrm((N_ODD, N_GROUPS_C, GMLP_CHUNK, GMLP_CHUNK), GMLP_CHUNK ** -0.5),
        "b_s": gain((N_ODD, N_GROUPS_C, GMLP_CHUNK)),
        "w_out_c": nrm((N_ODD, MIX_WIDTH, d), MIX_WIDTH ** -0.5),
        "w_xq": nrm((DEPTH, d, d), d ** -0.5),
        "w_xk": nrm((DEPTH, d, d), d ** -0.5),
        "w_xv": nrm((DEPTH, d, d), d ** -0.5),
        "w_xo": nrm((DEPTH, d, d), d ** -0.5),
    }


def reference(x, mem, norm_mix_g, norm_x_g, norm_mem_g, final_norm_g, w_in_ab, rel_bias,
              conv_w, conv_b, conv_ln_g, conv_ln_b, w_out_ab, w_in_c, sgu_ln_g, sgu_ln_b,
              w_s, b_s, w_out_c, w_xq, w_xk, w_xv, w_xo):
    h = x
    for layer in range(DEPTH):
        i = layer // 2
        hn = rmsnorm(h, norm_mix_g[layer])
        if layer % 2 == 0:
            y = ab_mixer(hn, w_in_ab[i], rel_bias[i], conv_w[i], conv_b[i],
                         conv_ln_g[i], conv_ln_b[i], w_out_ab[i])
        else:
            y = c_mixer(hn, w_in_c[i], sgu_ln_g[i], sgu_ln_b[i], w_s[i], b_s[i], w_out_c[i])
        h = h + y
        h = h + memory_cross_attention(rmsnorm(h, norm_x_g[layer]), rmsnorm(mem, norm_mem_g[layer]),
                                       w_xq[layer], w_xk[layer], w_xv[layer], w_xo[layer])
    return rmsnorm(h, final_norm_g)
```

```python
import numpy as np
from contextlib import ExitStack
import concourse.bass as bass
import concourse.mybir as mybir
from concourse.bass_utils import run_bass_kernel_spmd

F32 = mybir.dt.float32
BF16 = mybir.dt.bfloat16
AF = mybir.ActivationFunctionType
ALU = mybir.AluOpType

D = 2048
KC = 16
T = 256
SUB = T // 128
SEQ = 8192
NCORE = 8
TOK_CORE = 2048
NT = TOK_CORE // T
NH = 512 // T
NRING = 4 + SUB
EPS = 1e-6
NEG = -30000.0

ENGS = ["pe", "act", "dve", "pool", "sp"]
SEM_EPOCH = 8000

CV_MIX0, CV_MIX1, CV_X0, CV_X1, CV_M0, CV_M1, CV_FIN = 0, 16, 32, 48, 64, 80, 96
CV_CB, CV_CLG, CV_CLB = 112, 128, 144
CV_CW = 160
CV_SG, CV_SB = 656, 688
CV_RB = 720
NCV = 736


class Op:
    __slots__ = ("eng", "emit", "deps", "signal", "sem", "val", "is_dma", "stream")

    def __init__(self, eng, emit, is_dma=False, stream=None):
        self.eng = eng
        self.emit = emit
        self.deps = []
        self.signal = False
        self.sem = None
        self.val = None
        self.is_dma = is_dma
        self.stream = stream


class Prog:
    def __init__(self, nc):
        self.nc = nc
        self.ops = {e: [] for e in ENGS}
        self.last_w = {}
        self.readers = {}
        self.n_ops = 0
        self.last_s_dma = None

    def add(self, eng, emit, reads=(), writes=(), is_dma=False, stream=None):
        op = Op(eng, emit, is_dma, stream)
        if is_dma:
            op.signal = True
        ps_reads = [k for k in reads if isinstance(k, tuple) and k[0] == "ps"]
        if ps_reads:
            reads = [k for k in reads if not (isinstance(k, tuple) and k[0] == "ps")]
            writes = list(writes) + ps_reads
        deps = set()
        for k in reads:
            w = self.last_w.get(k)
            if w is not None:
                deps.add(w)
        for k in writes:
            w = self.last_w.get(k)
            if w is not None:
                deps.add(w)
            for r in self.readers.get(k, ()):
                deps.add(r)
        if eng == "pe":
            deps = {d for d in deps if d.eng != "pe"}
        op.deps = list(deps)
        for d in op.deps:
            d.signal = True
        for k in writes:
            self.last_w[k] = op
            self.readers[k] = []
        for k in reads:
            if k in writes:
                continue
            self.readers.setdefault(k, []).append(op)
        self.ops[eng].append(op)
        self.n_ops += 1
        return op

    def barrier(self):
        if getattr(self, 'no_barrier', False):
            return
        parts = ["pe", "act", "dve", "sp"]
        lasts = []
        for e in ["pe", "act", "dve"]:
            for op in reversed(self.ops[e]):
                if op.emit is not None:
                    lasts.append(op)
                    break
        if self.last_s_dma is not None:
            lasts.append(self.last_s_dma)
        for e in parts:
            op = Op(e, None)
            op.deps = [l for l in lasts if not (l.eng == e and not l.is_dma)]
            for d in op.deps:
                d.signal = True
            self.ops[e].append(op)

    def emit_all(self, sem_alloc, final_wait_ops=()):
        nc = self.nc
        for op in final_wait_ops:
            op.signal = True
        stream_sems = {}
        stream_cnt = {}
        for e in ENGS:
            cur_sem = None
            cnt = 0
            for op in self.ops[e]:
                if not op.signal:
                    continue
                if op.is_dma:
                    s = op.stream
                    if s not in stream_sems:
                        stream_sems[s] = sem_alloc()
                        stream_cnt[s] = 0
                    stream_cnt[s] += 16
                    op.sem = stream_sems[s]
                    op.val = stream_cnt[s]
                else:
                    if cur_sem is None or cnt >= SEM_EPOCH:
                        cur_sem = sem_alloc()
                        cnt = 0
                    cnt += 1
                    op.sem = cur_sem
                    op.val = cnt
        self.n_waits = 0
        with nc.Block() as block:
            def run(e):
                def body(eng):
                    waited = {}
                    for op in self.ops[e]:
                        need = {}
                        for d in op.deps:
                            key = id(d.sem)
                            if waited.get(key, 0) >= d.val:
                                continue
                            if key not in need or need[key][1] < d.val:
                                need[key] = (d.sem, d.val)
                        for key, (sem, val) in need.items():
                            eng.wait_ge(sem, val)
                            waited[key] = val
                            self.n_waits += 1
                        if op.emit is None:
                            continue
                        inst = op.emit(eng)
                        if op.signal:
                            inst.then_inc(op.sem, 16 if op.is_dma else 1)
                    if e == "sp":
                        for s_, sem in stream_sems.items():
                            if waited.get(id(sem), 0) < stream_cnt[s_]:
                                eng.wait_ge(sem, stream_cnt[s_])
                                waited[id(sem)] = stream_cnt[s_]
                return body
            block.tensor(run("pe"))
            block.scalar(run("act"))
            block.vector(run("dve"))
            block.gpsimd(run("pool"))
            block.sync(run("sp"))


def build_program(n_tiles=NT, do_l1=True, do_x=True, do_l0=True, dbg=None):
    dbg = dbg or {}
    nc = bass.Bass("TRN2", target_bir_lowering=False)
    ntok_all = (NH + n_tiles) * T
    x_d = nc.dram_tensor("x", [ntok_all, D], F32, kind="ExternalInput").ap()
    mem_d = nc.dram_tensor("mem", [256, D], F32, kind="ExternalInput").ap()
    cvec_d = nc.dram_tensor("cvec", [128, NCV], F32, kind="ExternalInput").ap()
    ident_d = nc.dram_tensor("ident", [128, 128], F32, kind="ExternalInput").ap()
    valid_d = nc.dram_tensor("valid", [128, 1], F32, kind="ExternalInput").ap()
    bm_d = nc.dram_tensor("biasmat", [16, 128, 384], F32, kind="ExternalInput").ap()
    wsT_d = nc.dram_tensor("wsT", [128, 8, 128], F32, kind="ExternalInput").ap()
    bs_d = nc.dram_tensor("b_s", [1, 1024], F32, kind="ExternalInput").ap()
    def wdecl(name, shape, used):
        if dbg.get("tiny_w"):
            used = False
        return nc.dram_tensor(name, shape if used else [1, 1], F32, kind="ExternalInput").ap()
    wtiny = nc.dram_tensor("wtiny", [4096, 512], F32, kind="ExternalInput").ap() if dbg.get("tiny_w") else None
    w_in_ab = wdecl("w_in_ab", [D, 14336], do_l0)
    w_out_ab = wdecl("w_out_ab", [4096, D], do_l0)
    w_in_c = wdecl("w_in_c", [D, 12288], do_l1)
    w_out_c = wdecl("w_out_c", [4096, D], do_l1)
    w_xq = wdecl("w_xq", [2, D, D], do_x)
    w_xk = wdecl("w_xk", [2, D, D], do_x)
    w_xv = wdecl("w_xv", [2, D, D], do_x)
    w_xo = wdecl("w_xo", [2, D, D], do_x)
    y_d = nc.dram_tensor("y", [n_tiles * T, D], F32, kind="ExternalOutput").ap()
    kx_d = nc.dram_tensor("kx_scr", [2, 128, 16 * 256], BF16, kind="Internal").ap()
    vx_d = nc.dram_tensor("vx_scr", [2, 128, 2 * 2048], BF16, kind="Internal").ap()
    N_SLABS = 84
    dg_d = nc.dram_tensor("dg_scr", [16, 128, 31 * 128], BF16, kind="Internal").ap()
    wscr_d = nc.dram_tensor("w_scr", [N_SLABS, 128, KC * 512], BF16, kind="Internal").ap()

    es = ExitStack()
    with es:
        def sb(name, shape, dt):
            return es.enter_context(nc.sbuf_tensor(name, shape, dt))

        hT = sb("hT", [128, KC, T], F32)
        hn = sb("hn", [128, KC, T], BF16)
        Yf = sb("Yf", [128, 16 * T], F32)
        Yb = Yf.bitcast(BF16)
        Kr = sb("Kr", [128, 16, NRING * 128], BF16)
        Vr = sb("Vr", [128, NRING, D], BF16)
        wbuf = [sb(f"wbuf{i}", [128, KC, 512], BF16) for i in range(3)]
        Sf = sb("Sf", [128, 6144], F32)
        Sb = Sf.bitcast(BF16)
        bmb = [sb(f"bm{i}", [128, 384], F32) for i in range(2)]
        cvec = sb("cvec_s", [128, NCV], F32)
        ident = sb("ident_s", [128, 128], F32)
        ones = sb("ones_s", [128, 128], BF16)
        epsb = sb("eps_s", [128, 1], F32)
        valid = sb("valid_s", [128, 1], F32)
        wsTf = sb("wsTf", [128, 8, 128], F32)
        wsT = sb("wsT_s", [128, 8, 128], BF16)
        Rm = sb("Rm", [128, 8, 128], F32)
        Bsm = sb("Bsm", [128, 8, 128], F32)
        sqt = [sb(f"sqt{i}", [128, T], BF16) for i in range(2)]
        rstd = sb("rstd", [128, T], F32)
        ybh = sb("ybh", [128, 16, 32], BF16)
        identb = sb("identb", [128, 128], BF16)
        dg = [sb(f"dg{i}", [128, 31, 128], BF16) for i in range(2)]
        PS = [es.enter_context(nc.psum_tensor(f"ps{i}", [128, 512], F32)) for i in range(8)]

        sems = []

        def sem_alloc():
            s = es.enter_context(nc.semaphore(f"sem{len(sems)}"))
            sems.append(s)
            return s

        P = Prog(nc)
        P.no_barrier = bool(dbg.get('no_barrier'))

        def ychunk(c):
            return Yb[:, c * T:(c + 1) * T]

        def stage(s):
            return Yf[:, s * 2048:(s + 1) * 2048]

        def stage_keys(s):
            n = 2048 * 4 // (T * 2)
            return [("Y", c) for c in range(s * n, (s + 1) * n)]

        bank_state = {"list": list(range(8)), "i": 0}

        def set_banks(lst):
            bank_state["list"] = list(lst)
            bank_state["i"] = 0

        def next_bank():
            b = bank_state["list"][bank_state["i"] % len(bank_state["list"])]
            bank_state["i"] += 1
            return b

        wstate = {"n": 0}

        slab_ids = {}

        def load_slab(w_ap, row0, col0, cache=True):
            slot = wstate["n"] % 3
            wstate["n"] += 1
            if wtiny is not None:
                w_ap = wtiny
                col0 = 0
            key = (w_ap.tensor.name, int(w_ap.offset), row0, col0)
            if cache and key in slab_ids:
                sid = slab_ids[key]
                P.add("pool", lambda e, slot=slot, sid=sid: e.dma_start(out=wbuf[slot][:].rearrange("p a b -> p (a b)"), in_=wscr_d[sid]),
                      reads=[("wscr", sid)], writes=[("w", slot)], is_dma=True, stream=f"w{slot}")
                return slot
            src = w_ap[row0:row0 + 2048, col0:col0 + 512].rearrange("(kc p) n -> p kc n", p=128)
            P.add("pool", lambda e, slot=slot, src=src: e.dma_start(out=wbuf[slot][:], in_=src),
                  writes=[("w", slot)], is_dma=True, stream=f"w{slot}")
            if cache and len(slab_ids) < N_SLABS:
                sid = len(slab_ids)
                slab_ids[key] = sid
                P.add("sp", lambda e, slot=slot, sid=sid: e.dma_start(out=wscr_d[sid], in_=wbuf[slot][:].rearrange("p a b -> p (a b)")),
                      reads=[("w", slot)], writes=[("wscr", sid)], is_dma=True, stream=f"ws{slot}")
            return slot

        def mm_group(out_ap, pskey, pairs, reads):
            n = len(pairs)
            for i, (l, r) in enumerate(pairs):
                rd = list(reads) if (i == 0 or i == n - 1) else []
                wr = [pskey] if (i == 0 or i == n - 1) else []
                P.add("pe", lambda e, l=l, r=r, i=i: e.matmul(out_ap, l, r, start=(i == 0), stop=(i == n - 1)),
                      reads=rd, writes=wr)

        def proj_fm_thunks(w_ap, row0, col0, rhs_fn, ntok, consume, banks=None, first=True, last=True, noc=4, cache=True):
            st_ = {}

            def mk(oc):
                def th():
                    if "slot" not in st_:
                        st_["slot"] = load_slab(w_ap, row0, col0, cache)
                    slot = st_["slot"]
                    b = banks[oc] if banks is not None else next_bank()
                    out_ap = PS[b][:, 0:ntok]
                    for kc in range(KC):
                        l = wbuf[slot][:, kc, oc * 128:(oc + 1) * 128]
                        r = rhs_fn(kc)[0]
                        st = first and kc == 0
                        sp = last and kc == KC - 1
                        edge = (kc == 0 or kc == KC - 1)
                        P.add("pe", lambda e, l=l, r=r, st=st, sp=sp, out_ap=out_ap: e.matmul(out_ap, l, r, start=st, stop=sp),
                              reads=[rhs_fn(kc)[1]] + ([("w", slot)] if edge else []), writes=[("ps", b)] if edge else [])
                    if last:
                        consume(oc, out_ap, ("ps", b))
                return th
            return [mk(oc) for oc in range(noc)]

        def proj_fm(*a, **k):
            for th in proj_fm_thunks(*a, **k):
                th()

        def proj_tm_thunks(w_ap, row0, col0, consume, cache=True):
            st_ = {}

            def mk(sub):
                def th():
                    if "slot" not in st_:
                        st_["slot"] = load_slab(w_ap, row0, col0, cache)
                    slot = st_["slot"]
                    b = next_bank()
                    out_ap = PS[b][:, 0:512]
                    for kc in range(KC):
                        l = hn[:, kc, sub * 128:(sub + 1) * 128]
                        r = wbuf[slot][:, kc, :]
                        edge = (kc == 0 or kc == KC - 1)
                        P.add("pe", lambda e, l=l, r=r, kc=kc, out_ap=out_ap: e.matmul(out_ap, l, r, start=(kc == 0), stop=(kc == KC - 1)),
                              reads=[("hn", kc)] + ([("w", slot)] if edge else []), writes=[("ps", b)] if edge else [])
                    consume(sub, out_ap, ("ps", b))
                return th
            return [mk(sub) for sub in range(SUB)]

        def proj_tm(*a, **k):
            for th in proj_tm_thunks(*a, **k):
                th()

        def hn_rhs(kc):
            return hn[:, kc, :], ("hn", kc)

        cp_state = {"n": 0}

        def copy_any(out, in_, reads, writes):
            cp_state["n"] += 1
            if cp_state["n"] % 2 == 0:
                P.add("act", lambda e: e.copy(out, in_), reads=reads, writes=writes)
            else:
                P.add("dve", lambda e: e.tensor_copy(out, in_), reads=reads, writes=writes)

        def load_tile(src_ap):
            for sub in range(SUB):
                s = sub % 2
                P.add("sp", lambda e, s=s, sub=sub: e.dma_start(out=stage(s), in_=src_ap[sub * 128:(sub + 1) * 128, :]),
                      writes=stage_keys(s), is_dma=True, stream=f"xin{s}")
                for kb in range(4):
                    b = next_bank()
                    for q in range(4):
                        kc = kb * 4 + q
                        edge = (q == 0 or q == 3)
                        P.add("pe", lambda e, b=b, q=q, kc=kc, s=s: e.transpose(PS[b][:, q * 128:(q + 1) * 128],
                                                                                stage(s)[:, kc * 128:(kc + 1) * 128], ident[:]),
                              reads=stage_keys(s) + ["ident"] if edge else [], writes=[("ps", b)] if edge else [])
                    copy_any(hT[:, kb * 4:kb * 4 + 4, sub * 128:(sub + 1) * 128],
                             PS[b][:, 0:512].rearrange("p (a b) -> p a b", a=4),
                             reads=[("ps", b)], writes=[("h", kb * 4 + q, sub) for q in range(4)])

        def hkeys(kc):
            return [("h", kc, sub) for sub in range(SUB)]

        def norm(gcol, out_fn=None, out_keys=None):
            b = next_bank()
            ss = PS[b][:, 0:T]
            for kc in range(KC):
                s = kc % 2
                P.add("act", lambda e, kc=kc, s=s: e.activation(sqt[s][:], hT[:, kc, :], AF.Square),
                      reads=hkeys(kc), writes=[("sqt", s)])
                P.add("pe", lambda e, kc=kc, s=s: e.matmul(ss, ones[:], sqt[s][:], start=(kc == 0), stop=(kc == KC - 1)),
                      reads=[("sqt", s), "ones"], writes=[("ps", b)] if (kc == 0 or kc == KC - 1) else [])
            P.add("act", lambda e: e.activation(rstd[:], ss, AF.Sqrt, bias=epsb[:], scale=1.0 / D),
                  reads=[("ps", b), "eps"], writes=["rstd"])
            P.add("dve", lambda e: e.reciprocal(rstd[:], rstd[:]), reads=["rstd"], writes=["rstd"])
            for kc in range(KC):
                if out_fn is None:
                    o, ok = hn[:, kc, :], [("hn", kc)]
                else:
                    o, ok = out_fn(kc), out_keys(kc)
                P.add("dve", lambda e, kc=kc, o=o: e.scalar_tensor_tensor(o, hT[:, kc, :], cvec[:, gcol + kc:gcol + kc + 1], rstd[:],
                                                                        ALU.mult, ALU.mult),
                      reads=hkeys(kc) + ["rstd", "cvec"], writes=ok)

        def resid_add(oc_global, ps_ap, pskey):
            P.add("dve", lambda e: e.tensor_tensor(hT[:, oc_global, :], hT[:, oc_global, :], ps_ap, ALU.add),
                  reads=hkeys(oc_global) + [pskey], writes=hkeys(oc_global))

        def out_proj(w_ap, ycs):
            nsl = len(ycs) // KC
            for og in range(4):
                banks = [next_bank() for _ in range(4)]
                for si in range(nsl):
                    def rf(kc, si=si):
                        c = ycs[si * KC + kc]
                        return ychunk(c), ("Y", c)
                    proj_fm(w_ap, si * 2048, og * 512, rf, T,
                            lambda oc, ps_ap, pskey, og=og: resid_add(og * 4 + oc, ps_ap, pskey),
                            banks=banks, first=(si == 0), last=(si == nsl - 1))

        P.add("sp", lambda e: e.dma_start(out=cvec[:], in_=cvec_d), writes=["cvec"], is_dma=True, stream="c0")
        P.add("sp", lambda e: e.dma_start(out=ident[:], in_=ident_d), writes=["ident"], is_dma=True, stream="c1")
        P.add("sp", lambda e: e.dma_start(out=valid[:], in_=valid_d), writes=["valid"], is_dma=True, stream="c2")
        P.add("sp", lambda e: e.dma_start(out=wsTf[:], in_=wsT_d), writes=["wsTf"], is_dma=True, stream="c3")
        P.add("sp", lambda e: e.dma_start(out=Bsm[:].rearrange("p a b -> p (a b)"), in_=bs_d.broadcast_to([128, 1024])),
              writes=["Bsm"], is_dma=True, stream="c4")
        P.add("dve", lambda e: e.memset(ones[:], 1.0), writes=["ones"])
        P.add("dve", lambda e: e.tensor_copy(identb[:], ident[:]), reads=["ident"], writes=["identb"])
        P.add("dve", lambda e: e.memset(epsb[:], EPS), writes=["eps"])
        P.add("dve", lambda e: e.tensor_copy(wsT[:], wsTf[:]), reads=["wsTf"], writes=["wsT"])
        P.add("dve", lambda e: e.memset(wsT[64:128, :, 0:64], 0.0), writes=["wsT"])
        for half in range(2):
            b = next_bank()
            P.add("pe", lambda e, b=b, half=half: e.matmul(PS[b][:, 0:512], ones[:],
                                                           wsT[:, half * 4:half * 4 + 4, :].rearrange("p a b -> p (a b)"),
                                                           start=True, stop=True),
                  reads=["wsT", "ones"], writes=[("ps", b)])
            P.add("dve", lambda e, b=b, half=half: e.tensor_copy(Rm[:, half * 4:half * 4 + 4, :].rearrange("p a b -> p (a b)"), PS[b][:, 0:512]),
                  reads=[("ps", b)], writes=["Rm"])

        if do_l0:
            for c in range(16):
                k = c % 2
                P.add("dve", lambda e, k=k, c=c: e.tensor_tensor(
                    dg[k][:], identb[:].unsqueeze(1).broadcast_to([128, 31, 128]),
                    cvec[:, CV_CW + c * 31:CV_CW + (c + 1) * 31].unsqueeze(2).broadcast_to([128, 31, 128]), ALU.mult),
                      reads=["identb", "cvec"], writes=[("dg", k)])
                P.add("sp", lambda e, k=k, c=c: e.dma_start(out=dg_d[c], in_=dg[k][:].rearrange("p a b -> p (a b)")),
                      reads=[("dg", k)], writes=[("dg_d", c)], is_dma=True, stream=f"dgs{k}")
        kxs = Sb[:, 0:4096].rearrange("p (a b) -> p a b", a=16)
        vxs = Sb[:, 4096:8192].rearrange("p (a b) -> p a b", a=2)
        if do_x:
            for l in range(2):
                load_tile(mem_d)
                norm(CV_M0 + 16 * l)
                for s in range(4):
                    proj_fm(w_xk[l], 0, s * 512, hn_rhs, T,
                            lambda oc, ps_ap, pskey, s=s: copy_any(kxs[:, s * 4 + oc, :], ps_ap, [pskey], [("kxs", s * 4 + oc)]), cache=False)
                for s in range(4):
                    proj_tm(w_xv[l], 0, s * 512,
                            lambda sub, ps_ap, pskey, s=s: copy_any(vxs[:, sub, s * 512:(s + 1) * 512], ps_ap, [pskey], [("vxs", sub, s)]), cache=False)
                op1 = P.add("sp", lambda e, l=l: e.dma_start(out=kx_d[l], in_=Sb[:, 0:4096]),
                            reads=[("kxs", c) for c in range(16)], writes=[("kx_d", l)], is_dma=True, stream="kvst")
                op2 = P.add("sp", lambda e, l=l: e.dma_start(out=vx_d[l], in_=Sb[:, 4096:8192]),
                            reads=[("vxs", sub, s) for sub in range(2) for s in range(4)], writes=[("vx_d", l)],
                            is_dma=True, stream="kvst2")
                P.last_s_dma = op2
                P.last_s_dma_b = op1
            P.barrier()
            for e in ["pe", "act", "dve", "sp"]:
                bop = P.ops[e][-1]
                if op1 not in bop.deps:
                    bop.deps.append(op1)
                    op1.signal = True

        def slot_of(t):
            return (t + 4 * NRING) % NRING

        SC_A = float(128 ** -0.5)
        SC_X = float(512 ** -0.5)
        out_ops = []

        def cross_attn(l):
            P.barrier()
            ld1 = P.add("sp", lambda e: e.dma_start(out=Sb[:, 0:4096], in_=kx_d[l]), reads=[("kx_d", l)],
                        writes=[("kxs", c) for c in range(16)], is_dma=True, stream="kvld")
            ld2 = P.add("sp", lambda e: e.dma_start(out=Sb[:, 4096:8192], in_=vx_d[l]), reads=[("vx_d", l)],
                        writes=[("vxs", sub, s) for sub in range(2) for s in range(4)], is_dma=True, stream="kvld2")
            Ex = [Sb[:, 8192 + i * 512: 8192 + (i + 1) * 512].rearrange("p (a b) -> p a b", a=2) for i in range(2)]
            rDx = [Sf[:, 4608 + i * 256: 4608 + (i + 1) * 256] for i in range(2)]
            norm(CV_X0 + 16 * l)
            set_banks(range(8))
            def x_qs(hx):
                proj_fm(w_xq[l], 0, hx * 512, hn_rhs, T,
                        lambda oc, ps_ap, pskey, hx=hx: copy_any(ychunk(hx * 4 + oc), ps_ap, [pskey], [("Y", hx * 4 + oc)]))
                es_ = hx % 2
                for kt in range(2):
                    b = next_bank()
                    pairs = [(kxs[:, hx * 4 + cc, kt * 128:(kt + 1) * 128], ychunk(hx * 4 + cc)) for cc in range(4)]
                    mm_group(PS[b][:, 0:T], ("ps", b), pairs,
                             [("kxs", hx * 4 + cc) for cc in range(4)] + [("Y", hx * 4 + cc) for cc in range(4)])
                    P.add("act", lambda e, b=b, kt=kt, es_=es_: e.activation(Ex[es_][:, kt, :], PS[b][:, 0:T], AF.Exp, scale=SC_X),
                          reads=[("ps", b)], writes=[("Ex", es_, kt)])

            def x_pv(hx):
                es_ = hx % 2
                bD = next_bank()
                mm_group(PS[bD][:, 0:T], ("ps", bD), [(ones[:], Ex[es_][:, kt, :]) for kt in range(2)],
                         [("Ex", es_, 0), ("Ex", es_, 1), "ones"])
                P.add("act", lambda e, bD=bD, es_=es_: e.activation(rDx[es_], PS[bD][:, 0:T], AF.Ln),
                      reads=[("ps", bD)], writes=[("rDx", es_)])
                P.add("act", lambda e, es_=es_: e.activation(rDx[es_], rDx[es_], AF.Exp, scale=-1.0),
                      reads=[("rDx", es_)], writes=[("rDx", es_)])
                for cc in range(4):
                    b = next_bank()
                    pairs = [(vxs[:, kt, hx * 512 + cc * 128: hx * 512 + (cc + 1) * 128], Ex[es_][:, kt, :]) for kt in range(2)]
                    mm_group(PS[b][:, 0:T], ("ps", b), pairs,
                             [("vxs", kt, hx) for kt in range(2)] + [("Ex", es_, 0), ("Ex", es_, 1)])
                    c = 16 + hx * 4 + cc
                    P.add("dve", lambda e, b=b, c=c, es_=es_: e.tensor_tensor(ychunk(c), PS[b][:, 0:T], rDx[es_], ALU.mult),
                          reads=[("ps", b), ("rDx", es_)], writes=[("Y", c)])

            x_qs(0)
            for hx in range(4):
                if hx + 1 < 4:
                    x_qs(hx + 1)
                x_pv(hx)
            out_proj(w_xo[l], list(range(16, 32)))

        def layer0_tile(i, halo):
            row0 = (i + NH) * T
            set_banks(range(8))
            load_tile(x_d[row0:row0 + T, :])
            if not dbg.get('no_norm0'):
                norm(CV_MIX0)
            P.barrier()
            qT4s = [[Sb[:, (st * 4 + oc) * T:(st * 4 + oc + 1) * T] for oc in range(4)] for st in range(2)]
            gate4s = [[Sb[:, 2048 + (st * 4 + oc) * T: 2048 + (st * 4 + oc + 1) * T] for oc in range(4)] for st in range(2)]
            E = [Sb[:, 4096 + k * 640: 4096 + (k + 1) * 640] for k in range(2)]
            tmpS = [Sf[:, 2688 + k * 384: 2688 + (k + 1) * 384] for k in range(2)]
            rD = [Sf[:, 3456 + k * 128: 3456 + (k + 1) * 128] for k in range(2)]
            g2 = [Sf[:, 3712 + k * 128: 3712 + (k + 1) * 128] for k in range(2)]
            if halo:
                set_banks(range(8))
            else:
                set_banks([6, 7])

            def hg_thunks(hg):
                st = hg % 2

                def k_cons(oc, ps_ap, pskey):
                    h = hg * 4 + oc
                    for sub in range(SUB):
                        sl = slot_of(SUB * i + sub)
                        copy_any(Kr[:, h, sl * 128:(sl + 1) * 128], ps_ap[:, sub * 128:(sub + 1) * 128],
                                 [pskey], [("K", h, sl)])

                def v_cons(sub, ps_ap, pskey):
                    sl = slot_of(SUB * i + sub)
                    copy_any(Vr[:, sl, hg * 512:(hg + 1) * 512], ps_ap, [pskey], [("V", sl, hg)])

                def q_cons(oc, ps_ap, pskey):
                    copy_any(qT4s[st][oc], ps_ap, [pskey], [("qT4", st, oc)])

                def g_cons(oc, ps_ap, pskey):
                    P.add("act", lambda e: e.activation(gate4s[st][oc], ps_ap, AF.Silu), reads=[pskey], writes=[("gate4", st, oc)])
                th = []
                th += proj_fm_thunks(w_in_ab, 0, 2048 + hg * 512, hn_rhs, T, k_cons)
                th += proj_tm_thunks(w_in_ab, 0, 4096 + hg * 512, v_cons)
                if not halo:
                    th += proj_fm_thunks(w_in_ab, 0, hg * 512, hn_rhs, T, q_cons)
                    th += proj_fm_thunks(w_in_ab, 0, 10240 + hg * 512, hn_rhs, T, g_cons)
                return th

            for th in hg_thunks(0):
                th()
            for hg in range(4):
                nxt = hg_thunks(hg + 1) if hg + 1 < 4 else []
                if halo:
                    for th in nxt:
                        th()
                    continue
                qT4 = qT4s[hg % 2]
                gate4 = gate4s[hg % 2]
                qkey = lambda hh, hg=hg: ("qT4", hg % 2, hh)
                gkey = lambda hh, hg=hg: ("gate4", hg % 2, hh)
                iters = [(hh, blk) for hh in range(0 if not dbg.get("skip_attn") else 4, 4) for blk in range(SUB)]
                jorder = [1, 2, 0, 3, 4]

                def a_ctx(n):
                    hh, blk = iters[n]
                    it = n % 2
                    Bq = SUB * i + blk
                    return dict(hh=hh, blk=blk, h=hg * 4 + hh, it=it, Bq=Bq, sa=0 + it, sbk=2 + it, od=4 + it,
                                slots={j: slot_of(Bq - 4 + j) for j in range(5)}, bs=(hg * 4 + hh) % 2)

                def a_scores(n):
                    c = a_ctx(n)
                    h, hh, blk, sa, sbk = c["h"], c["hh"], c["blk"], c["sa"], c["sbk"]
                    q_ap = qT4[hh][:, blk * 128:(blk + 1) * 128]
                    for pos, j in enumerate(jorder):
                        if pos < 2:
                            o = PS[sa][:, pos * 128:(pos + 1) * 128]
                            pk = ("ps", sa)
                        else:
                            o = PS[sbk][:, (pos - 2) * 128:(pos - 1) * 128]
                            pk = ("ps", sbk)
                        sl = c["slots"][j]
                        P.add("pe", lambda e, o=o, sl=sl, h=h, q_ap=q_ap: e.matmul(o, Kr[:, h, sl * 128:(sl + 1) * 128], q_ap,
                                                                              start=True, stop=True),
                              reads=[("K", h, sl), qkey(hh)], writes=[pk])

                def a_exp(n):
                    c = a_ctx(n)
                    h, hh, blk, it, sa, sbk, bs, Bq = c["h"], c["hh"], c["blk"], c["it"], c["sa"], c["sbk"], c["bs"], c["Bq"]
                    if blk == 0:
                        P.add("sp", lambda e, h=h, bs=bs: e.dma_start(out=bmb[bs][:], in_=bm_d[h]),
                              writes=[("bm", bs)], is_dma=True, stream=f"bm{bs}")
                        P.add("dve", lambda e, bs=bs: e.memset(bmb[bs][0:64, 64:128], NEG), writes=[("bm", bs)])
                        P.add("dve", lambda e, bs=bs: e.memset(bmb[bs][64:128, 256:320], NEG), writes=[("bm", bs)])
                    Ek = E[it]
                    P.add("act", lambda e, sa=sa, Ek=Ek, h=h: e.activation(Ek[:, 0:256], PS[sa][:, 0:256], AF.Exp,
                                                                        bias=cvec[:, CV_RB + h:CV_RB + h + 1], scale=SC_A),
                          reads=[("ps", sa), "cvec"], writes=[("E", it, 0)])
                    P.add("dve", lambda e, sbk=sbk, it=it, bs=bs: e.scalar_tensor_tensor(tmpS[it], PS[sbk][:, 0:384], SC_A, bmb[bs][:],
                                                                                       ALU.mult, ALU.add),
                          reads=[("ps", sbk), ("bm", bs)], writes=[("tmpS", it)])
                    P.add("act", lambda e, Ek=Ek, it=it: e.activation(Ek[:, 256:640], tmpS[it], AF.Exp),
                          reads=[("tmpS", it)], writes=[("E", it, 1)])
                    for pos, j in enumerate(jorder):
                        if Bq - 4 + j < 0:
                            grp = 0 if pos < 2 else 1
                            P.add("dve", lambda e, Ek=Ek, pos=pos: e.tensor_scalar(Ek[:, pos * 128:(pos + 1) * 128], Ek[:, pos * 128:(pos + 1) * 128],
                                                                                   valid[:, 0:1], None, ALU.mult),
                                  reads=[("E", it, grp), "valid"], writes=[("E", it, grp)])

                def a_pv(n):
                    c = a_ctx(n)
                    h, hh, blk, it, od, slots = c["h"], c["hh"], c["blk"], c["it"], c["od"], c["slots"]
                    Ek = E[it]
                    pairs = [(Vr[:, slots[j], h * 128:(h + 1) * 128], Ek[:, pos * 128:(pos + 1) * 128]) for pos, j in enumerate(jorder)]
                    mm_group(PS[od][:, 0:128], ("ps", od), pairs,
                             [("V", slots[j], hg) for j in range(5)] + [("E", it, 0), ("E", it, 1)])
                    pairs = [(ones[:], Ek[:, pos * 128:(pos + 1) * 128]) for pos in range(5)]
                    mm_group(PS[od][:, 128:256], ("ps", od), pairs, [("E", it, 0), ("E", it, 1), "ones"])
                    P.add("act", lambda e, od=od, it=it: e.activation(rD[it], PS[od][:, 128:256], AF.Ln),
                          reads=[("ps", od)], writes=[("rD", it)])
                    P.add("act", lambda e, it=it: e.activation(rD[it], rD[it], AF.Exp, scale=-1.0),
                          reads=[("rD", it)], writes=[("rD", it)])
                    g_ap = gate4[hh][:, blk * 128:(blk + 1) * 128]
                    P.add("dve", lambda e, it=it, g_ap=g_ap: e.tensor_tensor(g2[it], g_ap, rD[it], ALU.mult),
                          reads=[("rD", it), gkey(hh)], writes=[("g2", it)])
                    P.add("dve", lambda e, od=od, it=it, h=h, blk=blk: e.tensor_tensor(ychunk(h)[:, blk * 128:(blk + 1) * 128], PS[od][:, 0:128], g2[it], ALU.mult),
                          reads=[("ps", od), ("g2", it)], writes=[("Y", h)])

                per_it = -(-len(nxt) // len(iters))
                ti = 0
                a_scores(0)
                for n in range(len(iters)):
                    if n + 1 < len(iters):
                        a_scores(n + 1)
                    a_exp(n)
                    for _ in range(per_it):
                        if ti < len(nxt):
                            nxt[ti]()
                            ti += 1
                    a_pv(n)
                while ti < len(nxt):
                    nxt[ti]()
                    ti += 1
            P.barrier()
            set_banks(range(2, 8))
            W = T + 30
            ybT = [Sb[:, c * 288: c * 288 + W] for c in range(16)]
            a_tmp = [Sb[:, 4608 + oc * T: 4608 + (oc + 1) * T] for oc in range(4)]
            sig = [Sf[:, 2816 + k * T: 2816 + (k + 1) * T] for k in range(2)]
            acc = [Sf[:, 3328 + oc * T: 3328 + (oc + 1) * T] for oc in range(4)]
            mu = Sf[:, 4352:4352 + T]
            nmr = Sf[:, 4608:4608 + T]
            rs2 = Sf[:, 4864:4864 + T]
            t1b = [Sf[:, 5120 + k * T: 5120 + (k + 1) * T] for k in range(2)]
            gtb = [Sf[:, 5632 + k * T: 5632 + (k + 1) * T] for k in range(2)]
            if dbg.get("skip_B"):
                if not halo and not dbg.get("skip_wout"):
                    out_proj(w_out_ab, list(range(32)))
                return
            if halo:
                if i != -1 or dbg.get("skip_hglu"):
                    return
                def hn32(kc):
                    return hn[:, kc, T - 32:T], ("hn", kc)
                for s in range(4):
                    def a_cons(oc, ps_ap, pskey):
                        P.add("act", lambda e: e.copy(a_tmp[oc][:, 0:32], ps_ap), reads=[pskey], writes=[("a_tmp", oc)])
                    proj_fm(w_in_ab, 0, 6144 + s * 512, hn32, 32, a_cons)

                    def b_cons(oc, ps_ap, pskey, s=s):
                        k = oc % 2
                        c = s * 4 + oc
                        P.add("act", lambda e: e.activation(sig[k][:, 0:32], ps_ap, AF.Sigmoid), reads=[pskey], writes=[("sig", k)])
                        P.add("dve", lambda e: e.tensor_tensor(ybh[:, c, 0:30], a_tmp[oc][:, 2:32], sig[k][:, 2:32], ALU.mult),
                              reads=[("sig", k), ("a_tmp", oc)], writes=[("ybh", c)])
                    proj_fm(w_in_ab, 0, 8192 + s * 512, hn32, 32, b_cons)
                return
            for c in range(16):
                P.add("dve", lambda e, c=c: e.tensor_copy(ybT[c][:, 0:30], ybh[:, c, 0:30]), reads=[("ybh", c)], writes=[("yb", c)])
            ssum, ssq = PS[0][:, 0:T], PS[1][:, 0:T]
            for s in range(4):
                def a_cons(oc, ps_ap, pskey):
                    P.add("act", lambda e: e.copy(a_tmp[oc], ps_ap), reads=[pskey], writes=[("a_tmp", oc)])
                proj_fm(w_in_ab, 0, 6144 + s * 512, hn_rhs, T, a_cons)

                def b_cons(oc, ps_ap, pskey, s=s):
                    k = oc % 2
                    c = s * 4 + oc
                    P.add("act", lambda e: e.activation(sig[k], ps_ap, AF.Sigmoid), reads=[pskey], writes=[("sig", k)])
                    P.add("dve", lambda e: e.tensor_tensor(ybT[c][:, 30:30 + T], a_tmp[oc], sig[k], ALU.mult),
                          reads=[("sig", k), ("a_tmp", oc)], writes=[("yb", c)])
                proj_fm(w_in_ab, 0, 8192 + s * 512, hn_rhs, T, b_cons)
                for oc in range(4):
                    c = s * 4 + oc
                    k = c % 2
                    sq = c % 2
                    P.add("sp", lambda e, k=k, c=c: e.dma_start(out=dg[k][:].rearrange("p a b -> p (a b)"), in_=dg_d[c]),
                          reads=[("dg_d", c)], writes=[("dg", k)], is_dma=True, stream=f"dg{k}")
                    b = next_bank()
                    pairs = [(dg[k][:, j, :], ybT[c][:, j:j + T]) for j in range(31)]
                    mm_group(PS[b][:, 0:T], ("ps", b), pairs, [("dg", k), ("yb", c)])
                    P.add("dve", lambda e, c=c: e.tensor_copy(ybh[:, c, 0:30], ybT[c][:, T:T + 30]), reads=[("yb", c)], writes=[("ybh", c)])
                    P.add("act", lambda e, b=b, c=c: e.activation(ychunk(16 + c), PS[b][:, 0:T], AF.Identity,
                                                                 bias=cvec[:, CV_CB + c:CV_CB + c + 1], scale=1.0),
                          reads=[("ps", b), "cvec"], writes=[("Y", 16 + c)])
                    P.add("act", lambda e, b=b, c=c, sq=sq: e.activation(sqt[sq][:], PS[b][:, 0:T], AF.Square,
                                                                        bias=cvec[:, CV_CB + c:CV_CB + c + 1], scale=1.0),
                          reads=[("ps", b), "cvec"], writes=[("sqt", sq)])
                    P.add("pe", lambda e, c=c: e.matmul(ssum, ones[:], ychunk(16 + c), start=(c == 0), stop=(c == 15)),
                          reads=[("Y", 16 + c), "ones"], writes=[("ps", 0)] if c in (0, 15) else [])
                    P.add("pe", lambda e, c=c, sq=sq: e.matmul(ssq, ones[:], sqt[sq][:], start=(c == 0), stop=(c == 15)),
                          reads=[("sqt", sq), "ones"], writes=[("ps", 1)] if c in (0, 15) else [])
            P.add("dve", lambda e: e.tensor_scalar(mu, ssum, 1.0 / D, None, ALU.mult), reads=[("ps", 0)], writes=["mu"])
            P.add("dve", lambda e: e.tensor_tensor(nmr, mu, mu, ALU.mult), reads=["mu"], writes=["nmr"])
            P.add("dve", lambda e: e.scalar_tensor_tensor(rs2, ssq, 1.0 / D, nmr, ALU.mult, ALU.subtract),
                  reads=[("ps", 1), "nmr"], writes=["rs2"])
            P.add("act", lambda e: e.activation(rs2, rs2, AF.Sqrt, bias=epsb[:], scale=1.0), reads=["rs2", "eps"], writes=["rs2"])
            P.add("dve", lambda e: e.reciprocal(rs2, rs2), reads=["rs2"], writes=["rs2"])
            P.add("dve", lambda e: e.scalar_tensor_tensor(nmr, mu, -1.0, rs2, ALU.mult, ALU.mult), reads=["mu", "rs2"], writes=["nmr"])
            set_banks(range(8))
            for s in range(4):
                def gb_cons(oc, ps_ap, pskey, s=s):
                    c = s * 4 + oc
                    k = c % 2
                    P.add("act", lambda e: e.activation(gtb[k], ps_ap, AF.Silu), reads=[pskey], writes=[("gtb", k)])
                    P.add("dve", lambda e: e.tensor_tensor(t1b[k], ychunk(16 + c), rs2, ALU.mult),
                          reads=[("Y", 16 + c), "rs2"], writes=[("t1b", k)])
                    P.add("dve", lambda e: e.tensor_tensor(t1b[k], t1b[k], nmr, ALU.add), reads=[("t1b", k), "nmr"], writes=[("t1b", k)])
                    P.add("act", lambda e: e.activation(t1b[k], t1b[k], AF.Silu, bias=cvec[:, CV_CLB + c:CV_CLB + c + 1],
                                                        scale=cvec[:, CV_CLG + c:CV_CLG + c + 1]),
                          reads=[("t1b", k), "cvec"], writes=[("t1b", k)])
                    P.add("dve", lambda e: e.tensor_tensor(ychunk(16 + c), t1b[k], gtb[k], ALU.mult),
                          reads=[("t1b", k), ("gtb", k)], writes=[("Y", 16 + c)])
                proj_fm(w_in_ab, 0, 12288 + s * 512, hn_rhs, T, gb_cons)
            out_proj(w_out_ab, list(range(32)))

        def layer1_tile():
            set_banks(range(8))
            norm(CV_MIX1)
            P.barrier()
            vn = [Sb[:, sub * 4096:(sub + 1) * 4096] for sub in range(SUB)]
            sgs = [Sf[:, 4096 + oc * T: 4096 + (oc + 1) * T] for oc in range(4)]
            gt = [Sf[:, 5120 + k * T: 5120 + (k + 1) * T] for k in range(2)]
            t1 = [Sf[:, 5632 + k * 128: 5632 + (k + 1) * 128] for k in range(2)]
            bnst = [Sf[:, 5888 + sub * 48: 5888 + (sub + 1) * 48] for sub in range(SUB)]
            mv = [Sf[:, 5984 + sub * 2: 5984 + (sub + 1) * 2] for sub in range(SUB)]
            sdv = [Sf[:, 5992 + sub: 5993 + sub] for sub in range(SUB)]
            for s in range(8):
                def v_cons(sub, ps_ap, pskey, s=s):
                    P.add("dve", lambda e: e.bn_stats(bnst[sub][:, s * 6:(s + 1) * 6], ps_ap), reads=[pskey], writes=[("bnst", sub, s)])
                    P.add("act", lambda e: e.copy(vn[sub][:, s * 512:(s + 1) * 512], ps_ap), reads=[pskey], writes=[("vn", sub, s)])
                proj_tm(w_in_c, 0, 4096 + s * 512, v_cons)
            for sub in range(SUB):
                P.add("dve", lambda e, sub=sub: e.bn_aggr(mv[sub], bnst[sub]), reads=[("bnst", sub, s) for s in range(8)], writes=[("mv", sub)])
                P.add("act", lambda e, sub=sub: e.activation(sdv[sub], mv[sub][:, 1:2], AF.Sqrt, bias=epsb[:], scale=1.0),
                      reads=[("mv", sub), "eps"], writes=[("sdv", sub)])
                P.add("dve", lambda e, sub=sub: e.reciprocal(sdv[sub], sdv[sub]), reads=[("sdv", sub)], writes=[("sdv", sub)])
                for s in range(8):
                    P.add("dve", lambda e, sub=sub, s=s: e.tensor_scalar(vn[sub][:, s * 512:(s + 1) * 512], vn[sub][:, s * 512:(s + 1) * 512],
                                                                       mv[sub][:, 0:1], sdv[sub], ALU.subtract, ALU.mult),
                          reads=[("vn", sub, s), ("mv", sub), ("sdv", sub)], writes=[("vn", sub, s)])
            for s in range(8):
                g = s
                for oc in range(4):
                    cc = s * 4 + oc
                    b = next_bank()
                    for sub in range(SUB):
                        P.add("pe", lambda e, b=b, sub=sub, cc=cc, g=g: e.matmul(PS[b][:, sub * 128:(sub + 1) * 128],
                                                                              vn[sub][:, cc * 128:(cc + 1) * 128], wsT[:, g, :],
                                                                              start=True, stop=True),
                              reads=[("vn", sub, s), "wsT"], writes=[("ps", b)])
                    k = cc % 2
                    P.add("dve", lambda e, k=k, g=g, cc=cc: e.scalar_tensor_tensor(t1[k], Rm[:, g, :], cvec[:, CV_SB + cc:CV_SB + cc + 1], Bsm[:, g, :],
                                                                                 ALU.mult, ALU.add),
                          reads=["Rm", "Bsm", "cvec"], writes=[("t1", k)])
                    P.add("dve", lambda e, b=b, k=k, oc=oc, cc=cc: e.scalar_tensor_tensor(
                        sgs[oc].rearrange("p (a b) -> p a b", a=SUB), PS[b][:, 0:T].rearrange("p (a b) -> p a b", a=SUB),
                        cvec[:, CV_SG + cc:CV_SG + cc + 1], t1[k].unsqueeze(1).broadcast_to([128, SUB, 128]), ALU.mult, ALU.add),
                          reads=[("ps", b), ("t1", k), "cvec"], writes=[("sgs", oc)])

                def u_cons(oc, ps_ap, pskey):
                    P.add("dve", lambda e: e.tensor_tensor(sgs[oc], ps_ap, sgs[oc], ALU.mult), reads=[pskey, ("sgs", oc)], writes=[("sgs", oc)])
                proj_fm(w_in_c, 0, s * 512, hn_rhs, T, u_cons)

                def gc_cons(oc, ps_ap, pskey, s=s):
                    cc = s * 4 + oc
                    k = cc % 2
                    P.add("act", lambda e: e.activation(gt[k], ps_ap, AF.Silu), reads=[pskey], writes=[("gt", k)])
                    P.add("dve", lambda e: e.tensor_tensor(ychunk(cc), sgs[oc], gt[k], ALU.mult),
                          reads=[("sgs", oc), ("gt", k)], writes=[("Y", cc)])
                proj_fm(w_in_c, 0, 8192 + s * 512, hn_rhs, T, gc_cons)
            out_proj(w_out_c, list(range(32)))

        def final_tile(i):
            P.barrier()
            set_banks(range(8))
            fin = Sf[:, 0:KC * T].rearrange("p (a b) -> p a b", a=KC)
            norm(CV_FIN, out_fn=lambda kc: fin[:, kc, :], out_keys=lambda kc: [("fin", kc)])
            for sub in range(SUB):
                s = sub % 2
                for kb in range(4):
                    b = next_bank()
                    for q in range(4):
                        kc = kb * 4 + q
                        edge = (q == 0 or q == 3)
                        P.add("pe", lambda e, b=b, q=q, kc=kc, sub=sub: e.transpose(PS[b][:, q * 128:(q + 1) * 128],
                                                                                    fin[:, kc, sub * 128:(sub + 1) * 128], ident[:]),
                              reads=[("fin", kb * 4 + qq) for qq in range(4)] + ["ident"] if edge else [],
                              writes=[("ps", b)] if edge else [])
                    n = 2048 * 4 // (T * 2)
                    copy_any(stage(s)[:, kb * 512:(kb + 1) * 512], PS[b][:, 0:512], [("ps", b)],
                             [("Y", s * n + kb * (n // 4) + t_) for t_ in range(n // 4)])
                r0 = i * T + sub * 128
                op = P.add("sp", lambda e, s=s, r0=r0: e.dma_start(out=y_d[r0:r0 + 128, :], in_=stage(s)),
                           reads=stage_keys(s), writes=[("yout", i, sub)], is_dma=True, stream=f"out{s}")
                out_ops.append(op)

        for i in range(-NH, n_tiles):
            halo = i < 0
            if do_l0:
                layer0_tile(i, halo)
            else:
                if halo:
                    continue
                set_banks(range(8))
                load_tile(x_d[(i + NH) * T:(i + NH) * T + T, :])
            if halo:
                continue
            if do_x:
                cross_attn(0)
            if do_l1:
                layer1_tile()
                if do_x:
                    cross_attn(1)
            final_tile(i)

        P.emit_all(sem_alloc, final_wait_ops=out_ops)
        build_program.stats = (P.n_ops, P.n_waits, len(sems))
    return nc


def host_prep(inputs, n_tiles=NT):
    f = lambda a: np.ascontiguousarray(np.asarray(a, dtype=np.float32))
    x = f(inputs["x"])
    mem = f(inputs["mem"])

    def pc(v):
        v = f(v)
        return v.reshape(-1, 128).T

    cols = [pc(inputs["norm_mix_g"][0]), pc(inputs["norm_mix_g"][1]), pc(inputs["norm_x_g"][0]), pc(inputs["norm_x_g"][1]),
            pc(inputs["norm_mem_g"][0]), pc(inputs["norm_mem_g"][1]), pc(inputs["final_norm_g"]),
            pc(inputs["conv_b"][0]), pc(inputs["conv_ln_g"][0]), pc(inputs["conv_ln_b"][0])]
    cw = f(inputs["conv_w"][0])
    cw = cw.reshape(31, 16, 128).transpose(2, 1, 0).reshape(128, 16 * 31)
    cols.append(cw)
    cols.append(pc(inputs["sgu_ln_g"][0]))
    cols.append(pc(inputs["sgu_ln_b"][0]))
    rb = f(inputs["rel_bias"][0])
    cols.append(np.broadcast_to(rb[:, 256][None, :], (128, 16)))
    cvec = np.ascontiguousarray(np.concatenate(cols, axis=1).astype(np.float32))
    assert cvec.shape == (128, NCV), cvec.shape
    k = np.arange(128)[:, None]
    q = np.arange(128)[None, :]
    idx0 = np.full((128, 128), 256)
    idx3 = np.minimum(128 + q - k, 128) + 128
    idx4 = q - k + 128
    idx = np.concatenate([idx0, idx3, idx4], axis=1)
    biasmat = np.ascontiguousarray(rb[:, idx])
    ws = f(inputs["w_s"][0])
    wsT = np.ascontiguousarray(ws.transpose(2, 0, 1))
    b_s = np.ascontiguousarray(f(inputs["b_s"][0]).reshape(1, 1024))
    ident = np.eye(128, dtype=np.float32)
    shared = {
        "cvec": cvec, "ident": ident, "biasmat": biasmat, "wsT": wsT, "b_s": b_s,
        "w_in_ab": f(inputs["w_in_ab"][0]), "w_out_ab": f(inputs["w_out_ab"][0]),
        "w_in_c": f(inputs["w_in_c"][0]), "w_out_c": f(inputs["w_out_c"][0]),
        "w_xq": f(inputs["w_xq"]), "w_xk": f(inputs["w_xk"]), "w_xv": f(inputs["w_xv"]), "w_xo": f(inputs["w_xo"]),
    }
    in_maps = []
    for c in range(NCORE):
        b, qtr = c // 4, c % 4
        s0 = qtr * TOK_CORE
        ntok = n_tiles * T
        xa = np.zeros((512 + ntok, D), np.float32)
        if s0 > 0:
            xa[0:512] = x[b, s0 - 512:s0]
        xa[512:] = x[b, s0:s0 + ntok]
        m = dict(shared)
        m["x"] = xa
        m["mem"] = mem[b]
        m["valid"] = np.full((128, 1), 1.0 if s0 > 0 else 0.0, np.float32)
        in_maps.append(m)
    return in_maps


def kernel(**inputs):
    in_maps = host_prep(inputs)
    nc = build_program()
    res = run_bass_kernel_spmd(nc, in_maps, core_ids=list(range(NCORE)))
    out = np.empty((2, SEQ, D), np.float32)
    for c in range(NCORE):
        b, qtr = c // 4, c % 4
        out[b, qtr * TOK_CORE:(qtr + 1) * TOK_CORE] = res.results[c]["y"]
    return out
```
